# Optimizing a Trainium2 kernel written in Bass

```python
import jax, jax.numpy as jnp
from jax import lax
import numpy as np

D_MODEL = 4096
BATCH = 2
SEQ = 8192
DEPTH = 4

RWKV_HEAD = 64
RWKV_W = D_MODEL // 2
RWKV_HEADS = RWKV_W // RWKV_HEAD
DECAY_LORA = 96
ICLR_LORA = 96
GATE_LORA = 256
RWKV_CHUNK = 16
CONV_W = D_MODEL // 4
CONV_K = 31
SB_HEAD = 128
SB_W = D_MODEL - RWKV_W - CONV_W
SB_HEADS = SB_W // SB_HEAD
SB_BLOCK = 128
RWKV_COLS = 3 * RWKV_W + DECAY_LORA + ICLR_LORA + GATE_LORA
PROJ_COLS = RWKV_COLS + 2 * CONV_W + 3 * SB_W
D_FF = 2 * D_MODEL
FFN_CONV_K = 3
NORM_EPS = 1e-6
LN_EPS = 1e-5
GN_EPS = RWKV_HEAD * 1e-5

kernel_name = 'hybrid_rwkv7_conformer_stickbreak_trunk'


def rms_norm(x, g):
    xf = x.astype(jnp.float32)
    y = xf * lax.rsqrt(jnp.mean(xf * xf, axis=-1, keepdims=True) + NORM_EPS)
    return (y * g.astype(jnp.float32)).astype(x.dtype)


def causal_dwconv(x, w):
    k = w.shape[0]
    return lax.conv_general_dilated(
        x, w[:, None, :], window_strides=(1,), padding=[(k - 1, 0)],
        dimension_numbers=('NWC', 'WIO', 'NWC'), feature_group_count=x.shape[-1])


def token_shift(p, mu):
    prev = jnp.pad(p, ((0, 0), (1, 0), (0, 0)))[:, :-1]
    return p + mu * (prev - p)


def rwkv7_time_mix(p, mu, w0, w_up, a0, a_up, g_up, k_k, k_a, r_k, gn_g, gn_b):
    f32 = jnp.float32
    bsz, seq, _ = p.shape
    C, N, H = RWKV_CHUNK, RWKV_HEAD, RWKV_HEADS
    nc = seq // C
    p = token_shift(p, mu)
    r, k, v, w_lo, a_lo, g_lo = jnp.split(
        p, [RWKV_W, 2 * RWKV_W, 3 * RWKV_W, 3 * RWKV_W + DECAY_LORA,
            3 * RWKV_W + DECAY_LORA + ICLR_LORA], axis=-1)
    log_w = (-jax.nn.softplus(-(w0 + jnp.tanh(w_lo) @ w_up)) - 0.5).astype(f32)
    log_decay = -jnp.exp(log_w)
    iclr = jax.nn.sigmoid((a0 + a_lo @ a_up).astype(f32))
    g = (jax.nn.sigmoid(g_lo) @ g_up).astype(f32)

    def heads(t):
        return t.astype(f32).reshape(bsz, seq, H, N)

    kk = heads(k * k_k)
    kk = kk * lax.rsqrt(jnp.maximum(jnp.sum(kk * kk, axis=-1, keepdims=True), 1e-24))
    kt = heads(k.astype(f32) * (1.0 + (iclr - 1.0) * k_a.astype(f32)))
    rh, vh, lwh, ah = heads(r), heads(v), heads(log_decay), heads(iclr)

    def chunks(t):
        return t.reshape(bsz, nc, C, H, N).transpose(0, 3, 1, 2, 4)

    rc, kc, vc, lwc = chunks(rh), chunks(kt), chunks(vh), chunks(lwh)
    ac, bc = chunks(-kk), chunks(kk * ah)
    cum = jnp.cumsum(lwc, axis=3)
    tot = cum[:, :, :, -1:, :]
    a_t = ac * jnp.exp(cum - lwc)
    r_t = rc * jnp.exp(cum)
    inv = jnp.exp(-cum)
    b_hat, k_hat = bc * inv, kc * inv
    to_end = jnp.exp(tot - cum)
    b_end, k_end = bc * to_end, kc * to_end
    strict = jnp.asarray(np.tril(np.ones((C, C), np.float32), -1))
    incl = jnp.asarray(np.tril(np.ones((C, C), np.float32)))
    m_ab = jnp.einsum('bhntk,bhnik->bhnti', a_t, b_hat) * strict
    m_ak = jnp.einsum('bhntk,bhnik->bhnti', a_t, k_hat) * strict
    q_b = jnp.einsum('bhntk,bhnik->bhnti', r_t, b_hat) * incl
    q_k = jnp.einsum('bhntk,bhnik->bhnti', r_t, k_hat) * incl
    rhs = jnp.concatenate([a_t, jnp.einsum('bhnti,bhniv->bhntv', m_ak, vc)], axis=-1)
    sol = lax.linalg.triangular_solve(jnp.eye(C, dtype=f32) - m_ab, rhs, left_side=True,
                                      lower=True, unit_diagonal=True)
    wa, u0 = jnp.split(sol, [N], axis=-1)
    rq = r_t + jnp.einsum('bhnti,bhnik->bhntk', q_b, wa)
    o_in = (jnp.einsum('bhnti,bhniv->bhntv', q_b, u0)
            + jnp.einsum('bhnti,bhniv->bhntv', q_k, vc))
    trans = (jnp.einsum('bhnik,bhnij->bhnkj', wa, b_end)
             + jnp.exp(tot[:, :, :, 0, :])[..., None] * jnp.eye(N, dtype=f32))
    delta = (jnp.einsum('bhniv,bhnik->bhnvk', u0, b_end)
             + jnp.einsum('bhniv,bhnik->bhnvk', vc, k_end))

    def step(state, inp):
        rq_n, oin_n, tr_n, d_n = inp
        o = jnp.einsum('bhtk,bhvk->bhtv', rq_n, state) + oin_n
        state = jnp.einsum('bhvk,bhkj->bhvj', state, tr_n) + d_n
        return state, o

    xs = tuple(jnp.moveaxis(t, 2, 0) for t in (rq, o_in, trans, delta))
    s0 = jnp.zeros((bsz, H, N, N), f32)
    _, o = lax.scan(step, s0, xs)
    o = o.transpose(1, 0, 3, 2, 4).reshape(bsz, seq, H, N)
    mean = jnp.mean(o, axis=-1, keepdims=True)
    var = jnp.mean(jnp.square(o - mean), axis=-1, keepdims=True)
    o = ((o - mean) * lax.rsqrt(var + GN_EPS)).reshape(bsz, seq, RWKV_W)
    o = o * gn_g.astype(f32) + gn_b.astype(f32)
    bonus = jnp.sum(rh * kt * r_k.astype(f32), axis=-1, keepdims=True) * vh
    o = o + bonus.reshape(bsz, seq, RWKV_W)
    return (o * g).astype(p.dtype)


def conformer_conv(p, conv_w, conv_b, ln_g, ln_b):
    f32 = jnp.float32
    val, gate = jnp.split(p, 2, axis=-1)
    u = causal_dwconv(val * jax.nn.sigmoid(gate), conv_w) + conv_b
    uf = u.astype(f32)
    mean = jnp.mean(uf, axis=-1, keepdims=True)
    var = jnp.mean(jnp.square(uf - mean), axis=-1, keepdims=True)
    y = (uf - mean) * lax.rsqrt(var + LN_EPS) * ln_g.astype(f32) + ln_b.astype(f32)
    return jax.nn.silu(y).astype(p.dtype)


def stick_breaking_attn(p, q_g, k_g):
    f32 = jnp.float32
    bsz, seq, _ = p.shape
    BLK = SB_BLOCK
    q, k, v = jnp.split(p, 3, axis=-1)

    def heads(t):
        return t.reshape(bsz, seq, SB_HEADS, SB_HEAD)

    q = rms_norm(heads(q), q_g).astype(f32).transpose(0, 2, 1, 3) * (SB_HEAD ** -0.5)
    k = rms_norm(heads(k), k_g).astype(f32).transpose(0, 2, 1, 3)
    v = heads(v).astype(f32).transpose(0, 2, 1, 3)
    nb = seq // BLK
    after_in = jnp.asarray(np.tril(np.ones((BLK, BLK), np.float32), -1))
    after_blk = jnp.asarray(np.tril(np.ones((nb, nb), np.float32), -1))
    outs = []
    for b in range(nb):
        nk = b + 1
        kl = nk * BLK
        z = jnp.einsum('bhqd,bhkd->bhqk', q[:, :, b * BLK:(b + 1) * BLK], k[:, :, :kl])
        mask = jnp.asarray(np.arange(kl)[None, :] < (b * BLK + np.arange(BLK))[:, None])
        l = jnp.where(mask, jax.nn.log_sigmoid(-z), 0.0).reshape(bsz, SB_HEADS, BLK, nk, BLK)
        after = (jnp.einsum('bhqnj,js->bhqns', l, after_in)
                 + jnp.einsum('bhqm,mn->bhqn', jnp.sum(l, axis=-1), after_blk[:nk, :nk])[..., None])
        att = jnp.where(mask, jnp.exp(jax.nn.log_sigmoid(z)
                                      + after.reshape(bsz, SB_HEADS, BLK, kl)), 0.0)
        outs.append(jnp.einsum('bhqk,bhkd->bhqd', att, v[:, :, :kl]))
    o = jnp.concatenate(outs, axis=2)
    o = o.transpose(0, 2, 1, 3).reshape(bsz, seq, SB_W)
    return o.astype(p.dtype)


def setup_inputs(seed: int = 0) -> dict:
    key = jax.random.key(seed)
    ks = jax.random.split(key, 26)
    f32 = jnp.float32
    L = DEPTH

    def nrm(k, shape, s):
        return jax.random.normal(k, shape, f32) * s

    return {
        'x': nrm(ks[0], (BATCH, SEQ, D_MODEL), 1.0),
        'w_in': nrm(ks[1], (L, D_MODEL, PROJ_COLS), D_MODEL ** -0.5),
        'w_out': nrm(ks[2], (L, D_MODEL, D_MODEL), D_MODEL ** -0.5),
        'attn_norm': 1.0 + nrm(ks[3], (L, D_MODEL), 0.1),
        'ffn_norm': 1.0 + nrm(ks[4], (L, D_MODEL), 0.1),
        'rwkv_mu': jax.random.uniform(ks[5], (L, RWKV_COLS), f32),
        'rwkv_w0': nrm(ks[6], (L, RWKV_W), 0.5),
        'rwkv_w_up': nrm(ks[7], (L, DECAY_LORA, RWKV_W), 0.1),
        'rwkv_a0': nrm(ks[8], (L, RWKV_W), 0.5),
        'rwkv_a_up': nrm(ks[9], (L, ICLR_LORA, RWKV_W), 0.5 * ICLR_LORA ** -0.5),
        'rwkv_g_up': nrm(ks[10], (L, GATE_LORA, RWKV_W), GATE_LORA ** -0.5),
        'rwkv_k_k': 0.85 + nrm(ks[11], (L, RWKV_W), 0.1),
        'rwkv_k_a': 1.0 + nrm(ks[12], (L, RWKV_W), 0.1),
        'rwkv_r_k': nrm(ks[13], (L, RWKV_HEADS, RWKV_HEAD), 0.1),
        'rwkv_gn_g': 1.0 + nrm(ks[14], (L, RWKV_W), 0.1),
        'rwkv_gn_b': nrm(ks[15], (L, RWKV_W), 0.01),
        'conv_w': nrm(ks[16], (L, CONV_K, CONV_W), CONV_K ** -0.5),
        'conv_b': nrm(ks[17], (L, CONV_W), 0.01),
        'conv_ln_g': 1.0 + nrm(ks[18], (L, CONV_W), 0.1),
        'conv_ln_b': nrm(ks[19], (L, CONV_W), 0.01),
        'sb_q_norm': 1.0 + nrm(ks[20], (L, SB_HEAD), 0.1),
        'sb_k_norm': 1.0 + nrm(ks[21], (L, SB_HEAD), 0.1),
        'ffn_up': nrm(ks[22], (L, D_MODEL, 2 * D_FF), D_MODEL ** -0.5),
        'ffn_conv': nrm(ks[23], (L, FFN_CONV_K, 2 * D_FF), FFN_CONV_K ** -0.5),
        'ffn_down': nrm(ks[24], (L, D_FF, D_MODEL), D_FF ** -0.5),
    }


def reference(x, w_in, w_out, attn_norm, ffn_norm, rwkv_mu, rwkv_w0, rwkv_w_up, rwkv_a0,
              rwkv_a_up, rwkv_g_up, rwkv_k_k, rwkv_k_a, rwkv_r_k, rwkv_gn_g, rwkv_gn_b,
              conv_w, conv_b, conv_ln_g, conv_ln_b, sb_q_norm, sb_k_norm,
              ffn_up, ffn_conv, ffn_down):
    for l in range(DEPTH):
        h = rms_norm(x, attn_norm[l])
        p = h @ w_in[l]
        p_a, p_b, p_c = jnp.split(p, [RWKV_COLS, RWKV_COLS + 2 * CONV_W], axis=-1)
        y_a = rwkv7_time_mix(p_a, rwkv_mu[l], rwkv_w0[l], rwkv_w_up[l], rwkv_a0[l],
                             rwkv_a_up[l], rwkv_g_up[l], rwkv_k_k[l], rwkv_k_a[l],
                             rwkv_r_k[l], rwkv_gn_g[l], rwkv_gn_b[l])
        y_b = conformer_conv(p_b, conv_w[l], conv_b[l], conv_ln_g[l], conv_ln_b[l])
        y_c = stick_breaking_attn(p_c, sb_q_norm[l], sb_k_norm[l])
        y = jnp.concatenate([y_a, y_b, y_c], axis=-1)
        x = x + y @ w_out[l]
        h = rms_norm(x, ffn_norm[l])
        u = causal_dwconv(h @ ffn_up[l], ffn_conv[l])
        gate, val = jnp.split(u, 2, axis=-1)
        x = x + (jax.nn.silu(gate) * val) @ ffn_down[l]
    return x
```

```python
import contextlib
import math
import numpy as np
import concourse.bass as bass
import concourse.mybir as mybir
from concourse.bass_utils import run_bass_kernel_spmd

F32 = mybir.dt.float32
BF16 = mybir.dt.bfloat16
AF = mybir.ActivationFunctionType
ALU = mybir.AluOpType

import os as _os
SAME_ENGINE_SYNC = _os.environ.get("SES", "1") == "1"
NORM_EPS = 1e-6
LN_EPS = 1e-5
GN_EPS = 64 * 1e-5
CDEC = -math.exp(-0.5)


class Prog:
    ENGS = ['pe', 'act', 'dve', 'pool', 'sp']

    def __init__(self, nc, st, n_dma=16):
        self.nc = nc
        self.ops = {e: [] for e in self.ENGS}
        self.count = {e: 0 for e in self.ENGS}
        self.known = {e: {} for e in self.ENGS}
        self.last_w = {}
        self.readers = {}
        self.n_dma = n_dma
        self.dma_tot = [0] * n_dma
        self.dma_rr = 0
        self.nops = 0
        self.excl = set()
        self.sems = {}
        for e in ['pe', 'act', 'dve', 'pool']:
            self.sems[e] = st.enter_context(nc.semaphore("s_" + e))
        for i in range(n_dma):
            self.sems[('dma', i)] = st.enter_context(nc.semaphore("s_dma%d" % i))

    def op(self, eng, fn, reads=(), writes=(), dma=False):
        waits = {}
        known = self.known[eng]
        xr = [k for k in reads if k in self.excl]
        if xr:
            writes = list(writes) + xr
            reads = [k for k in reads if k not in self.excl]

        def need(sk, val, teng, same_ok):
            if teng == eng and sk == eng:
                if eng == 'pe' or same_ok or not SAME_ENGINE_SYNC:
                    return
            if known.get(sk, 0) >= val:
                return
            if waits.get(sk, 0) < val:
                waits[sk] = val

        for k in reads:
            for sk, (val, teng) in self.last_w.get(k, {}).items():
                need(sk, val, teng, False)
        for k in writes:
            for sk, (val, teng) in self.last_w.get(k, {}).items():
                need(sk, val, teng, True)
            for sk, (val, teng) in self.readers.get(k, {}).items():
                need(sk, val, teng, True)
        if dma:
            i = self.dma_rr
            self.dma_rr = (i + 1) % self.n_dma
            sk = ('dma', i)
            if self.dma_tot[i] > 0:
                need(sk, self.dma_tot[i], None, False)
            self.dma_tot[i] += 16
            tok = (sk, self.dma_tot[i], eng)
        else:
            self.count[eng] += 1
            tok = (eng, self.count[eng], eng)
        for sk, v in waits.items():
            known[sk] = v
        self.ops[eng].append((waits, fn, tok, dma))
        for k in writes:
            self.last_w[k] = {tok[0]: (tok[1], tok[2])}
            self.readers[k] = {}
        for k in reads:
            self.readers.setdefault(k, {})[tok[0]] = (tok[1], tok[2])
        self.nops += 1
        return tok

    def barrier(self):
        allw = {}
        for e in ['pe', 'act', 'dve', 'pool']:
            if self.count[e] > 0:
                allw[e] = self.count[e]
        for i in range(self.n_dma):
            if self.dma_tot[i] > 0:
                allw[('dma', i)] = self.dma_tot[i]
        for e in self.ENGS:
            waits = {}
            for sk, v in allw.items():
                if sk == e:
                    continue
                if self.known[e].get(sk, 0) < v:
                    waits[sk] = v
                    self.known[e][sk] = v
            if waits:
                self.ops[e].append((waits, None, None, False))
        self.last_w = {}
        self.readers = {}

    def flush(self):
        nc = self.nc
        sems = self.sems
        ops = self.ops
        with nc.Block() as block:
            def replay(name, eng):
                for waits, fn, tok, dma in ops[name]:
                    for sk, v in waits.items():
                        eng.wait_ge(sems[sk], v)
                    if fn is None:
                        continue
                    ins = fn(eng)
                    ins.then_inc(sems[tok[0]], 16 if dma else 1)

            @block.tensor
            def _(eng):
                replay('pe', eng)

            @block.scalar
            def _(eng):
                replay('act', eng)

            @block.vector
            def _(eng):
                replay('dve', eng)

            @block.gpsimd
            def _(eng):
                replay('pool', eng)

            @block.sync
            def _(eng):
                replay('sp', eng)
        self.ops = {e: [] for e in self.ENGS}


class V:
    __slots__ = ('ap', 'key')

    def __init__(self, ap, key):
        self.ap = ap
        self.key = key

    def f(self, fn):
        return V(fn(self.ap), self.key)


class Tl:
    def __init__(self, h, key):
        self.h = h
        self.key = key

    def __getitem__(self, idx):
        return V(self.h[idx], self.key)


class Tv:
    def __init__(self, ap, key):
        self.h = ap
        self.key = key

    def __getitem__(self, idx):
        return V(self.h[idx], self.key)


class B:
    def __init__(self, nc, st):
        self.nc = nc
        self.P = Prog(nc, st)
        self.uid = 0

    def sb(self, st, name, shape, dt):
        self.uid += 1
        nm = "%s_%d" % (name, self.uid)
        return Tl(st.enter_context(self.nc.sbuf_tensor(nm, list(shape), dt)), nm)

    def ps(self, st, name, shape, dt=F32):
        self.uid += 1
        nm = "%s_%d" % (name, self.uid)
        self.P.excl.add(nm)
        return Tl(st.enter_context(self.nc.psum_tensor(nm, list(shape), dt)), nm)

    def mm(self, out, lhsT, rhs, start=True, stop=True):
        self.P.op('pe', lambda e: e.matmul(out.ap, lhsT.ap, rhs.ap, start=start, stop=stop),
                  reads=[lhsT.key, rhs.key], writes=[out.key])

    def tr(self, out, in_, ident):
        self.P.op('pe', lambda e: e.transpose(out.ap, in_.ap, ident.ap),
                  reads=[in_.key, ident.key], writes=[out.key])

    def tt(self, eng, out, a, b, op):
        self.P.op(eng, lambda e: e.tensor_tensor(out.ap, a.ap, b.ap, op),
                  reads=[a.key, b.key], writes=[out.key])

    def ts(self, eng, out, a, s1, s2, op0, op1=None):
        rk = [a.key]
        s1a = s1.ap if isinstance(s1, V) else s1
        s2a = s2.ap if isinstance(s2, V) else s2
        if isinstance(s1, V):
            rk.append(s1.key)
        if isinstance(s2, V):
            rk.append(s2.key)
        if op1 is None:
            self.P.op(eng, lambda e: e.tensor_scalar(out.ap, a.ap, s1a, None, op0),
                      reads=rk, writes=[out.key])
        else:
            self.P.op(eng, lambda e: e.tensor_scalar(out.ap, a.ap, s1a, s2a, op0, op1),
                      reads=rk, writes=[out.key])

    def stt(self, out, a, s, b, op0, op1):
        rk = [a.key, b.key]
        sa = s.ap if isinstance(s, V) else s
        if isinstance(s, V):
            rk.append(s.key)
        self.P.op('dve', lambda e: e.scalar_tensor_tensor(out.ap, a.ap, sa, b.ap, op0, op1),
                  reads=rk, writes=[out.key])

    def scan(self, out, d0, d1, init, op0, op1):
        self.P.op('dve', lambda e: e.tensor_tensor_scan(out.ap, d0.ap, d1.ap, init, op0, op1),
                  reads=[d0.key, d1.key], writes=[out.key])

    def cp(self, eng, out, a):
        if eng == 'act':
            self.P.op('act', lambda e: e.copy(out.ap, a.ap), reads=[a.key], writes=[out.key])
        else:
            self.P.op(eng, lambda e: e.tensor_copy(out.ap, a.ap), reads=[a.key], writes=[out.key])

    def act(self, out, a, func, bias=None, scale=None):
        rk = [a.key]
        kw = {}
        if bias is not None:
            if isinstance(bias, V):
                rk.append(bias.key)
                kw['bias'] = bias.ap
            else:
                kw['bias'] = bias
        if scale is not None:
            kw['scale'] = scale
        self.P.op('act', lambda e: e.activation(out.ap, a.ap, func, **kw), reads=rk, writes=[out.key])

    def rsqrt(self, out, a):
        self.act(out, a, AF.Sqrt)
        self.P.op('dve', lambda e: e.reciprocal(out.ap, out.ap), reads=[out.key], writes=[out.key])

    def memset(self, eng, out, c):
        self.P.op(eng, lambda e: e.memset(out.ap, c), writes=[out.key])

    def dma_kc(self, out_fn, in_fn, nkc, okey, ikey, step=8):
        for k0 in range(0, nkc, step):
            k1 = min(nkc, k0 + step)
            self.dma(V(out_fn(k0, k1), okey), V(in_fn(k0, k1), ikey))

    def dma(self, out, in_, eng='sp'):
        self.P.op(eng, lambda e: e.dma_start(out=out.ap, in_=in_.ap), reads=[in_.key], writes=[out.key], dma=True)


class Cfg:
    def __init__(self, D, T):
        self.D, self.T = D, T
        self.KC = D // 128
        self.RW = D // 2
        self.H = self.RW // 64
        self.CW = D // 4
        self.SW = D - self.RW - self.CW
        self.SH = self.SW // 128
        self.DL, self.IL, self.GL = 96, 96, 256
        self.RC = 3 * self.RW + self.DL + self.IL + self.GL
        self.PC = self.RC + 2 * self.CW + 3 * self.SW
        self.DFF = 2 * D
        self.o_wlo = 3 * self.RW
        self.o_alo = self.o_wlo + self.DL
        self.o_glo = self.o_alo + self.IL
        self.o_cv = self.RC
        self.o_cg = self.RC + self.CW
        self.o_sq = self.RC + 2 * self.CW
        self.o_sk = self.o_sq + self.SW
        self.o_sv = self.o_sk + self.SW


def const_arrays():
    c = {}
    c['ident'] = np.eye(128, dtype=np.float32)
    c['ones'] = np.ones((128, 128), np.float32)
    i = np.arange(128)
    same = (i[:, None] // 64) == (i[None, :] // 64)
    c['m_sl'] = ((i[:, None] > i[None, :]) & same).astype(np.float32)
    su = ((i[:, None] < i[None, :]) & same).astype(np.float32)
    ui = ((i[:, None] <= i[None, :]) & same).astype(np.float32)
    c['m_2'] = np.concatenate([su, ui], 1)
    t = np.arange(512)
    c['cmask'] = np.tile((t % 64 != 0).astype(np.float32)[None, :], (64, 1))
    sbm = np.zeros((128, 4, 512), np.float32)
    for b in range(4):
        sbm[:, b, :] = ((128 * b + i)[:, None] < t[None, :])
    c['sbmask'] = sbm.reshape(128, 2048)
    c['tri'] = (i[:, None] >= i[None, :]).astype(np.float32)
    c['mrow'] = np.stack([(i // 64 == 0), (i // 64 == 1)], 1).astype(np.float32)
    return c


CONST_SHAPES = {'ident': [128, 128], 'ones': [128, 128], 'm_sl': [128, 128], 'm_2': [128, 256],
                'cmask': [64, 512], 'sbmask': [128, 2048], 'tri': [128, 128], 'mrow': [128, 2]}


def param_shapes(cf):
    H, KC = cf.H, cf.KC
    return {
        'an': [128, KC], 'fn': [128, KC],
        'mu_rkv': [64, 3 * H], 'mu_w': [96, 1], 'mu_a': [96, 1], 'mu_g': [128, 2],
        'w0': [64, H], 'a0': [64, H], 'k_k': [64, H], 'k_a': [64, H], 'r_k': [64, H],
        'gn_g': [64, H], 'gn_b': [64, H],
        'w_up': [96, cf.RW], 'a_up': [96, cf.RW], 'g_up': [128, 2 * cf.RW],
        'cv_w': [128, (cf.CW // 128) * 31], 'cv_b': [128, cf.CW // 128],
        'ln_g': [128, cf.CW // 128], 'ln_b': [128, cf.CW // 128],
        'sq_g': [128, 1], 'sk_g': [128, 1],
        'fc': [128, (2 * cf.DFF // 128) * 3],
    }


def prep_params(cf, l, inp):
    H, KC = cf.H, cf.KC
    f = lambda a: np.ascontiguousarray(np.asarray(a, dtype=np.float32))
    col = lambda v, n: f(np.asarray(v).reshape(n, -1).T)
    mu = np.asarray(inp['rwkv_mu'][l])
    NCc = cf.CW // 128
    d = {
        'an': col(inp['attn_norm'][l], KC), 'fn': col(inp['ffn_norm'][l], KC),
        'mu_rkv': col(mu[:3 * cf.RW], 3 * H),
        'mu_w': f(mu[cf.o_wlo:cf.o_wlo + 96].reshape(96, 1)),
        'mu_a': f(mu[cf.o_alo:cf.o_alo + 96].reshape(96, 1)),
        'mu_g': col(mu[cf.o_glo:cf.o_glo + 256], 2),
        'w0': col(inp['rwkv_w0'][l], H), 'a0': col(inp['rwkv_a0'][l], H),
        'k_k': col(inp['rwkv_k_k'][l], H), 'k_a': col(inp['rwkv_k_a'][l], H),
        'r_k': col(np.asarray(inp['rwkv_r_k'][l]).reshape(-1), H),
        'gn_g': col(inp['rwkv_gn_g'][l], H), 'gn_b': col(inp['rwkv_gn_b'][l], H),
        'w_up': f(inp['rwkv_w_up'][l]), 'a_up': f(inp['rwkv_a_up'][l]),
        'g_up': f(np.asarray(inp['rwkv_g_up'][l]).reshape(2, 128, cf.RW).transpose(1, 0, 2).reshape(128, 2 * cf.RW)),
        'cv_w': f(np.asarray(inp['conv_w'][l]).T.reshape(NCc, 128, 31).transpose(1, 0, 2).reshape(128, NCc * 31)),
        'cv_b': col(inp['conv_b'][l], NCc), 'ln_g': col(inp['conv_ln_g'][l], NCc),
        'ln_b': col(inp['conv_ln_b'][l], NCc),
        'sq_g': f(np.asarray(inp['sb_q_norm'][l]).reshape(128, 1)),
        'sk_g': f(np.asarray(inp['sb_k_norm'][l]).reshape(128, 1)),
        'fc': f(np.asarray(inp['ffn_conv'][l]).T.reshape(2 * cf.DFF // 128, 128, 3).transpose(1, 0, 2).reshape(128, -1)),
    }
    return d


def build(cf, depth, debug=False, stages=None):
    D, T, KC = cf.D, cf.T, cf.KC
    nc = bass.Bass("TRN2", target_bir_lowering=False)
    dt_in = lambda n, s: nc.dram_tensor(n, list(s), F32, kind="ExternalInput").ap()
    x_in = dt_in("x", [T, D])
    out_d = nc.dram_tensor("out", [T, D], F32, kind="ExternalOutput").ap()
    Wd = []
    for l in range(depth):
        w = {'w_in': dt_in("w_in%d" % l, [D, cf.PC]), 'w_out': dt_in("w_out%d" % l, [D, D]),
             'ffn_up': dt_in("ffn_up%d" % l, [D, 2 * cf.DFF]), 'ffn_down': dt_in("ffn_down%d" % l, [cf.DFF, D])}
        for n, s in param_shapes(cf).items():
            w[n] = dt_in("%s%d" % (n, l), s)
        Wd.append(w)
    cd = {n: dt_in("c_" + n, s) for n, s in CONST_SHAPES.items()}
    xT = nc.dram_tensor("xT", [D, T], F32).ap()
    hT = nc.dram_tensor("hT", [D, T], BF16).ap()
    class _PT:
        def __init__(self):
            self.secs = [(0, cf.RC, nc.dram_tensor("pTa", [cf.RC, T], F32).ap()),
                         (cf.RC, cf.o_sq, nc.dram_tensor("pTb", [2 * cf.CW, T], F32).ap()),
                         (cf.o_sq, cf.PC, nc.dram_tensor("pTc", [3 * cf.SW, T], F32).ap())]

        def __getitem__(self, idx):
            rs, cs = idx
            for lo, hi, ap in self.secs:
                if lo <= rs.start and rs.stop <= hi:
                    return ap[rs.start - lo:rs.stop - lo, cs]
            raise ValueError("pT slice straddles sections")
    pT = _PT()
    yT = nc.dram_tensor("yT", [D, T], BF16).ap()
    hidT = nc.dram_tensor("hidT", [cf.DFF, T], BF16).ap()
    dbg = {}
    if debug:
        dbg['pT'] = nc.dram_tensor("d_pT", [cf.PC, T], F32, kind="ExternalOutput").ap()
        dbg['yT'] = nc.dram_tensor("d_yT", [D, T], F32, kind="ExternalOutput").ap()
        dbg['x1T'] = nc.dram_tensor("d_x1T", [D, T], F32, kind="ExternalOutput").ap()

    with contextlib.ExitStack() as st0:
        b = B(nc, st0)
        P = b.P
        C = {}
        for n, s in CONST_SHAPES.items():
            C[n] = b.sb(st0, "c_" + n, s, F32)
            b.dma(C[n][:], V(cd[n], "in_c_" + n))
        identb = b.sb(st0, "identb", [128, 128], BF16)
        b.cp('dve', identb[:], C['ident'][:])
        onesb = b.sb(st0, "onesb", [128, 128], BF16)
        b.cp('dve', onesb[:], C['ones'][:])
        trib = b.sb(st0, "trib", [128, 128], BF16)
        b.cp('dve', trib[:], C['tri'][:])
        ctx = dict(cf=cf, b=b, nc=nc, C=C, identb=identb, onesb=onesb, trib=trib,
                   xT=xT, hT=hT, pT=pT, yT=yT, hidT=hidT)

        ph_transpose_in(ctx, x_in)
        for l in range(depth):
            with contextlib.ExitStack() as stl:
                PR = {}
                for n, s in param_shapes(cf).items():
                    if n in ('w_up', 'a_up', 'g_up'):
                        continue
                    PR[n] = b.sb(stl, "p_" + n, s, F32)
                    b.dma(PR[n][:], V(Wd[l][n], "in_" + n))
                ctx['PR'] = PR
                ctx['W'] = Wd[l]
                P.barrier()
                ph_rmsnorm(ctx, PR['an'])
                ph_gemm(ctx, "inproj", hT, Wd[l]['w_in'], D, inproj_blocks(cf), 'copy_pT')
                if debug:
                    for lo, hi, ap in pT.secs:
                        ph_copy_dram(ctx, ap, dbg['pT'][lo:hi, :], hi - lo, F32)
                if stages is None or 'conv' in stages:
                    ph_conv(ctx)
                if stages is None or 'sb' in stages:
                    ph_sb(ctx)
                if stages is None or 'rwkv' in stages:
                    ph_rwkv(ctx)
                if debug:
                    ph_copy_dram(ctx, yT, dbg['yT'], D, BF16)
                ph_gemm(ctx, "outproj", yT, Wd[l]['w_out'], D, [(i * 128, 128) for i in range(KC)], 'resid')
                if debug:
                    ph_copy_dram(ctx, xT, dbg['x1T'], D, F32)
                ph_rmsnorm(ctx, PR['fn'])
                ph_gemm(ctx, "up", hT, Wd[l]['ffn_up'], D, up_blocks(cf), 'ffn')
                ph_gemm(ctx, "down", hidT, Wd[l]['ffn_down'], cf.DFF, [(i * 128, 128) for i in range(KC)], 'resid')
                P.flush()
        ph_transpose_out(ctx, out_d)
        P.barrier()
        P.flush()
    return nc


def inproj_blocks(cf):
    blks = []
    for s in range(3):
        for i in range(cf.RW // 128):
            blks.append((s * cf.RW + i * 128, 128))
    blks.append((cf.o_wlo, 96))
    blks.append((cf.o_alo, 96))
    blks.append((cf.o_glo, 128))
    blks.append((cf.o_glo + 128, 128))
    for o in range(cf.o_cv, cf.PC, 128):
        blks.append((o, 128))
    return blks


def up_blocks(cf):
    blks = []
    for j in range(cf.DFF // 128):
        blks.append((j * 128, 128))
        blks.append((cf.DFF + j * 128, 128))
    return blks


def ph_copy_dram(ctx, src, dst, rows, dt):
    b, cf = ctx['b'], ctx['cf']
    T = cf.T
    b.P.barrier()
    with contextlib.ExitStack() as st:
        ta = [b.sb(st, "cpa", [128, 1024], dt) for _ in range(2)]
        tb = [b.sb(st, "cpb", [128, 1024], F32) for _ in range(2)]
        n = 0
        for r0 in range(0, rows, 128):
            nr = min(128, rows - r0)
            for c0 in range(0, T, 1024):
                t1 = ta[n % 2]
                t2 = tb[n % 2]
                n += 1
                b.dma(t1[0:nr, :], V(src[r0:r0 + nr, c0:c0 + 1024], ("cps", r0, c0)))
                if dt == BF16:
                    b.cp('dve', t2[0:nr, :], t1[0:nr, :])
                    t1 = t2
                b.dma(V(dst[r0:r0 + nr, c0:c0 + 1024], ("cpd", r0, c0)), t1[0:nr, :])
        b.P.barrier()
        b.P.flush()


def ph_transpose_in(ctx, x_in):
    b, cf, C = ctx['b'], ctx['cf'], ctx['C']
    D, T, KC = cf.D, cf.T, cf.KC
    xT = ctx['xT']
    with contextlib.ExitStack() as st:
        xin = [b.sb(st, "xin", [128, D], F32) for _ in range(2)]
        xo = [b.sb(st, "xo", [128, KC, 128], F32) for _ in range(2)]
        pss = [b.ps(st, "trp", [128, 4, 128]) for _ in range(2)]
        n = 0
        for tt in range(T // 128):
            xi = xin[tt % 2]
            o = xo[tt % 2]
            b.dma(xi[:], V(x_in[tt * 128:(tt + 1) * 128, :], ("xin", tt)))
            for g in range(KC // 4):
                ps = pss[n % 2]
                n += 1
                for j in range(4):
                    kc = g * 4 + j
                    b.tr(ps[:, j, :], xi[:, kc * 128:(kc + 1) * 128], C['ident'][:])
                b.cp('dve' if g % 2 == 0 else 'act', o[:, g * 4:(g + 1) * 4, :], ps[:])
            xTr = xT.rearrange("(kc p) t -> p kc t", p=128)
            b.dma_kc(lambda a, c: xTr[:, a:c, tt * 128:(tt + 1) * 128], lambda a, c: o.h[:, a:c, :], KC, ("xT", tt), o.key)
        b.P.barrier()
        b.P.flush()


def ph_transpose_out(ctx, out_d):
    b, cf, C = ctx['b'], ctx['cf'], ctx['C']
    D, T, KC = cf.D, cf.T, cf.KC
    xT = ctx['xT']
    b.P.barrier()
    with contextlib.ExitStack() as st:
        xin = [b.sb(st, "xi2", [128, KC, 128], F32) for _ in range(2)]
        xo = [b.sb(st, "xo2", [128, D], F32) for _ in range(2)]
        pss = [b.ps(st, "trp2", [128, 4, 128]) for _ in range(2)]
        n = 0
        for tt in range(T // 128):
            xi = xin[tt % 2]
            o = xo[tt % 2]
            xTr = xT.rearrange("(kc p) t -> p kc t", p=128)
            b.dma_kc(lambda a, c: xi.h[:, a:c, :], lambda a, c: xTr[:, a:c, tt * 128:(tt + 1) * 128], KC, xi.key, ("xT", tt))
            for g in range(KC // 4):
                ps = pss[n % 2]
                n += 1
                for j in range(4):
                    kc = g * 4 + j
                    b.tr(ps[:, j, :], xi[:, kc, :], C['ident'][:])
                b.cp('dve' if g % 2 == 0 else 'act',
                     o[:, g * 512:(g + 1) * 512].f(lambda a: a.rearrange("p (j f) -> p j f", j=4)), ps[:])
            b.dma(V(out_d[tt * 128:(tt + 1) * 128, :], ("out", tt)), o[:])
        b.P.barrier()
        b.P.flush()


def ph_rmsnorm(ctx, g):
    b, cf, C = ctx['b'], ctx['cf'], ctx['C']
    D, T, KC = cf.D, cf.T, cf.KC
    xT, hT = ctx['xT'], ctx['hT']
    TT = 512
    b.P.barrier()
    with contextlib.ExitStack() as st:
        xs = [b.sb(st, "rx", [128, KC, TT], F32) for _ in range(1)]
        hs = [b.sb(st, "rh", [128, KC, TT], BF16) for _ in range(2)]
        sq = [b.sb(st, "rsq", [128, TT], F32) for _ in range(2)]
        rstd = b.sb(st, "rstd", [128, TT], F32)
        ps = b.ps(st, "rps", [128, TT])
        for tt in range(T // TT):
            x = xs[0]
            h = hs[tt % 2]
            tok = slice(tt * TT, (tt + 1) * TT)
            xTr = xT.rearrange("(kc p) t -> p kc t", p=128)
            hTr = hT.rearrange("(kc p) t -> p kc t", p=128)
            b.dma_kc(lambda a, c: x.h[:, a:c, :], lambda a, c: xTr[:, a:c, tok], KC, x.key, ("xT", tt))
            for kc in range(KC):
                s = sq[kc % 2]
                b.act(s[:], x[:, kc, :], AF.Square)
                b.mm(ps[:], C['ones'][:], s[:], start=(kc == 0), stop=(kc == KC - 1))
            b.ts('dve', rstd[:], ps[:], 1.0 / D, NORM_EPS, ALU.mult, ALU.add)
            b.rsqrt(rstd[:], rstd[:])
            for kc in range(KC):
                b.stt(h[:, kc, :], x[:, kc, :], g[:, kc:kc + 1], rstd[:], ALU.mult, ALU.mult)
            b.dma_kc(lambda a, c: hTr[:, a:c, tok], lambda a, c: h.h[:, a:c, :], KC, ("hT", tt), h.key)
        b.P.barrier()
        b.P.flush()


def ph_gemm(ctx, name, actT, Wdram, K, blocks, epi):
    b, cf, C, PR = ctx['b'], ctx['cf'], ctx['C'], ctx['PR']
    T = cf.T
    KCn = K // 128
    TS = min(T, 1024 if K <= 4096 else 512)
    NJ = TS // 512
    xT, pT, hidT = ctx['xT'], ctx['pT'], ctx['hidT']
    b.P.barrier()
    with contextlib.ExitStack() as st:
        act = b.sb(st, "g_act", [128, KCn, TS], BF16)
        wst = [b.sb(st, "g_wst", [128, KCn, 128], F32) for _ in range(2 if KCn <= 32 else 1)]
        wbf = [b.sb(st, "g_wbf", [128, KCn, 128], BF16) for _ in range(2)]
        pss = [b.ps(st, "g_ps", [128, 512]) for _ in range(4)]
        ob = [b.sb(st, "g_ob", [128, 512], F32) for _ in range(4)]
        if epi == 'ffn':
            NB = len(blocks)
            halo = b.sb(st, "f_halo", [128, NB, 2], F32)
            b.memset('pool', halo[:], 0.0)
            ub = [b.sb(st, "f_ub", [128, 2 + 512], F32) for _ in range(2)]
            cb = [b.sb(st, "f_cb", [128, 512], F32) for _ in range(2)]
            gs = b.sb(st, "f_gs", [128, NJ, 512], F32)
            hb = [b.sb(st, "f_hb", [128, 512], BF16) for _ in range(2)]
        n = 0
        wi = 0
        for ts_ in range(T // TS):
            t0 = ts_ * TS
            aTr = actT.rearrange("(kc p) t -> p kc t", p=128)
            b.dma_kc(lambda a, c: act.h[:, a:c, :], lambda a, c: aTr[:, a:c, t0:t0 + TS], KCn, act.key, (name + "a", ts_))
            for bi, (c0, ncol) in enumerate(blocks):
                ws, wb = wst[wi % len(wst)], wbf[wi % 2]
                wi += 1
                Wr = Wdram.rearrange("(kc p) n -> p kc n", p=128)
                b.dma_kc(lambda a, c: ws.h[:, a:c, 0:ncol], lambda a, c: Wr[:, a:c, c0:c0 + ncol], KCn, ws.key, (name + "w", bi))
                b.cp('pool', wb[:, :, 0:ncol], ws[:, :, 0:ncol])
                for j in range(NJ):
                    ps = pss[n % 4]
                    o = ob[n % 4]
                    n += 1
                    for kc in range(KCn):
                        b.mm(ps[0:ncol, :], wb[:, kc, 0:ncol], act[:, kc, j * 512:(j + 1) * 512],
                             start=(kc == 0), stop=(kc == KCn - 1))
                    tok = slice(t0 + j * 512, t0 + (j + 1) * 512)
                    tk = (ts_, j)
                    if epi == 'copy_pT':
                        b.cp('act' if n % 2 else 'dve', o[0:ncol, :], ps[0:ncol, :])
                        b.dma(V(pT[c0:c0 + ncol, tok], ("pT", bi, tk)), o[0:ncol, :])
                    elif epi == 'resid':
                        b.dma(o[:], V(xT[c0:c0 + 128, tok], ("xTb", bi, tk)))
                        b.tt('dve', o[:], o[:], ps[:], ALU.add)
                        b.dma(V(xT[c0:c0 + 128, tok], ("xTb", bi, tk)), o[:])
                    elif epi == 'ffn':
                        u = ub[n % 2]
                        c = cb[n % 2]
                        fc = PR['fc']
                        ch = c0 // 128
                        b.cp('act', u[:, 2:514], ps[:])
                        b.cp('pool', u[:, 0:2], halo[:, bi, :])
                        b.cp('pool', halo[:, bi, :], u[:, 512:514])
                        b.ts('dve', c[:], u[:, 2:514], fc[:, ch * 3 + 2:ch * 3 + 3], None, ALU.mult)
                        b.stt(c[:], u[:, 1:513], fc[:, ch * 3 + 1:ch * 3 + 2], c[:], ALU.mult, ALU.add)
                        b.stt(c[:], u[:, 0:512], fc[:, ch * 3:ch * 3 + 1], c[:], ALU.mult, ALU.add)
                        if bi % 2 == 0:
                            b.act(gs[:, j, :], c[:], AF.Silu)
                        else:
                            h_ = hb[(n // 2) % 2]
                            b.tt('dve', h_[:], c[:], gs[:, j, :], ALU.mult)
                            r0 = (bi // 2) * 128
                            b.dma(V(hidT[r0:r0 + 128, tok], ("hid", bi, tk)), h_[:])
        b.P.barrier()
        b.P.flush()


def ph_conv(ctx):
    b, cf, C, PR = ctx['b'], ctx['cf'], ctx['C'], ctx['PR']
    T = cf.T
    NCc = cf.CW // 128
    pT, yT = ctx['pT'], ctx['yT']
    TC = 512
    b.P.barrier()
    with contextlib.ExitStack() as st:
        vb = [b.sb(st, "cv_v", [128, 30 + TC], F32) for _ in range(2)]
        gb = [b.sb(st, "cv_g", [128, 30 + TC], F32) for _ in range(2)]
        glu = [b.sb(st, "cv_glu", [128, 30 + TC], F32) for _ in range(2)]
        accA = b.sb(st, "cv_aa", [128, TC], F32)
        accB = b.sb(st, "cv_ab", [128, TC], F32)
        tmpB = b.sb(st, "cv_tb", [128, TC], F32)
        u = b.sb(st, "cv_u", [128, NCc, TC], F32)
        sq = b.sb(st, "cv_sq", [128, TC], F32)
        mean = b.sb(st, "cv_mean", [128, TC], F32)
        msq = b.sb(st, "cv_msq", [128, TC], F32)
        rstd = b.sb(st, "cv_rstd", [128, TC], F32)
        t1 = b.sb(st, "cv_t1", [128, TC], F32)
        yb = [b.sb(st, "cv_y", [128, TC], BF16) for _ in range(2)]
        ps_s = b.ps(st, "cv_pss", [128, TC])
        ps_q = b.ps(st, "cv_psq", [128, TC])
        cw = PR['cv_w']
        n = 0
        for tt in range(T // TC):
            t0 = tt * TC
            for c in range(NCc):
                v_, g_, gl = vb[n % 2], gb[n % 2], glu[n % 2]
                n += 1
                rv = cf.o_cv + c * 128
                rg = cf.o_cg + c * 128
                if tt == 0:
                    b.memset('pool', v_[:, 0:30], 0.0)
                    b.memset('pool', g_[:, 0:30], 0.0)
                    b.dma(v_[:, 30:], V(pT[rv:rv + 128, 0:TC], ("pTv", c, tt)))
                    b.dma(g_[:, 30:], V(pT[rg:rg + 128, 0:TC], ("pTg", c, tt)))
                else:
                    b.dma(v_[:], V(pT[rv:rv + 128, t0 - 30:t0 + TC], ("pTv", c, tt)))
                    b.dma(g_[:], V(pT[rg:rg + 128, t0 - 30:t0 + TC], ("pTg", c, tt)))
                b.act(g_[:], g_[:], AF.Sigmoid)
                b.tt('dve', gl[:], v_[:], g_[:], ALU.mult)
                wk = lambda k: cw[:, c * 31 + k:c * 31 + k + 1]
                b.ts('dve', accA[:], gl[:, 30:30 + TC], wk(30), None, ALU.mult)
                b.ts('pool', accB[:], gl[:, 29:29 + TC], wk(29), None, ALU.mult)
                for k in range(29):
                    if k % 2 == 0:
                        b.stt(accA[:], gl[:, k:k + TC], wk(k), accA[:], ALU.mult, ALU.add)
                    else:
                        b.ts('pool', tmpB[:], gl[:, k:k + TC], wk(k), None, ALU.mult)
                        b.tt('pool', accB[:], accB[:], tmpB[:], ALU.add)
                b.stt(u[:, c, :], accA[:], PR['cv_b'][:, c:c + 1], accB[:], ALU.add, ALU.add)
                b.mm(ps_s[:], C['ones'][:], u[:, c, :], start=(c == 0), stop=(c == NCc - 1))
                b.act(sq[:], u[:, c, :], AF.Square)
                b.mm(ps_q[:], C['ones'][:], sq[:], start=(c == 0), stop=(c == NCc - 1))
            b.ts('dve', mean[:], ps_s[:], 1.0 / cf.CW, None, ALU.mult)
            b.tt('dve', msq[:], mean[:], mean[:], ALU.mult)
            b.stt(rstd[:], ps_q[:], 1.0 / cf.CW, msq[:], ALU.mult, ALU.subtract)
            b.ts('dve', rstd[:], rstd[:], LN_EPS, None, ALU.add)
            b.rsqrt(rstd[:], rstd[:])
            for c in range(NCc):
                y_ = yb[c % 2]
                b.tt('dve', t1[:], u[:, c, :], mean[:], ALU.subtract)
                b.tt('dve', t1[:], t1[:], rstd[:], ALU.mult)
                b.ts('dve', t1[:], t1[:], PR['ln_g'][:, c:c + 1], PR['ln_b'][:, c:c + 1], ALU.mult, ALU.add)
                b.act(y_[:], t1[:], AF.Silu)
                r0 = cf.RW + c * 128
                b.dma(V(yT[r0:r0 + 128, t0:t0 + TC], ("yTc", c, tt)), y_[:])
        b.P.barrier()
        b.P.flush()


def ph_sb(ctx):
    b, cf, C, PR = ctx['b'], ctx['cf'], ctx['C'], ctx['PR']
    T = cf.T
    pT, yT = ctx['pT'], ctx['yT']
    onesb, trib = ctx['onesb'], ctx['trib']
    NQ = T // 512
    NB = T // 128
    b.P.barrier()
    with contextlib.ExitStack() as st:
        qn = b.sb(st, "sb_qn", [128, T], BF16)
        kn = b.sb(st, "sb_kn", [128, T], BF16)
        vt = b.sb(st, "sb_vt", [128, NB, 128], BF16)
        ld = [b.sb(st, "sb_ld", [128, 512], F32) for _ in range(2)]
        sq = b.sb(st, "sb_sq", [128, 512], F32)
        rs = b.sb(st, "sb_rs", [128, 512], F32)
        zs = [b.sb(st, "sb_zs", [128, 512], F32) for _ in range(2)]
        e1 = [b.sb(st, "sb_e1", [128, 512], F32) for _ in range(2)]
        sp = [b.sb(st, "sb_sp", [128, 512], F32) for _ in range(2)]
        spb = [b.sb(st, "sb_spb", [128, 512], BF16) for _ in range(2)]
        ssum = b.sb(st, "sb_ssum", [128, 512], F32)
        ssumb = [b.sb(st, "sb_ssumb", [128, 512], BF16) for _ in range(2)]
        arg = [b.sb(st, "sb_arg", [128, 512], F32) for _ in range(2)]
        E = [b.sb(st, "sb_E", [128, 512], BF16) for _ in range(2)]
        ot = [b.sb(st, "sb_ot", [128, 512], BF16) for _ in range(2)]
        ps_z = [b.ps(st, "sb_pz", [128, 512]) for _ in range(2)]
        ps_a = [b.ps(st, "sb_pa", [128, 512]) for _ in range(2)]
        ps_o = b.ps(st, "sb_po", [128, 512])
        ps_n = b.ps(st, "sb_pn", [128, 512])
        sbm = C['sbmask']
        n = 0
        for h in range(cf.SH):
            rq, rk, rv = cf.o_sq + h * 128, cf.o_sk + h * 128, cf.o_sv + h * 128
            for tt in range(NQ):
                tok = slice(tt * 512, (tt + 1) * 512)
                for which, r0, gsc, dst, extra in ((0, rq, PR['sq_g'], qn, 128.0 ** -0.5), (1, rk, PR['sk_g'], kn, 1.0)):
                    l_ = ld[n % 2]
                    n += 1
                    b.dma(l_[:], V(pT[r0:r0 + 128, tok], ("pTs", which, h, tt)))
                    b.act(sq[:], l_[:], AF.Square)
                    b.mm(ps_n[:], C['ones'][:], sq[:])
                    b.ts('dve', rs[:], ps_n[:], 1.0 / 128, NORM_EPS, ALU.mult, ALU.add)
                    b.rsqrt(rs[:], rs[:])
                    if extra != 1.0:
                        b.ts('dve', rs[:], rs[:], extra, None, ALU.mult)
                    b.stt(dst[:, tok], l_[:], gsc[:, 0:1], rs[:], ALU.mult, ALU.mult)
                l_ = ld[n % 2]
                n += 1
                b.dma(l_[:], V(pT[rv:rv + 128, tok], ("pTs", 2, h, tt)))
                for j in range(4):
                    b.tr(ps_n[:, j * 128:(j + 1) * 128], l_[:, j * 128:(j + 1) * 128], C['ident'][:])
                b.cp('act', vt[:, tt * 4:(tt + 1) * 4, :], ps_n[:].f(lambda a: a.rearrange("p (j f) -> p j f", j=4)))
            import os
            SBLIM = int(os.environ.get("SBLIM", "99"))
            PE_ = os.environ.get("SBPOOL", "pool")
            LNB = C['ones'][:, 0:1] if os.environ.get("SBLN", "f") == "ap" else 1.0
            STEP = int(os.environ.get("SBSTEP", "9"))
            for Q in range(min(NQ, SBLIM)):
                qtok = slice(Q * 512, (Q + 1) * 512)
                nkb = 4 * Q + 4
                first = True
                for kb in range(nkb - 1, max(-1, nkb - 1 - int(os.environ.get("SBKB", "999"))), -1):
                    i = n % 2 if os.environ.get("SBI0", "0") == "0" else 0
                    n += 1
                    diag = kb >= 4 * Q
                    pz, pa = ps_z[i], ps_a[i]
                    b.mm(pz[:], kn[:, kb * 128:(kb + 1) * 128], qn[:, qtok])
                    b.cp('act', zs[i][:], pz[:])
                    if STEP < 2:
                        continue
                    b.act(e1[i][:], zs[i][:], AF.Exp)
                    if diag:
                        m = sbm[:, (kb - 4 * Q) * 512:(kb - 4 * Q + 1) * 512]
                        b.act(sp[i][:], e1[i][:], AF.Ln, bias=LNB)
                        b.tt(PE_, sp[i][:], sp[i][:], m, ALU.mult)
                        b.cp(PE_, spb[i][:], sp[i][:])
                    else:
                        b.act(sp[i][:], e1[i][:], AF.Ln, bias=LNB)
                        b.cp(PE_, spb[i][:], sp[i][:])
                    if STEP < 3:
                        continue
                    b.mm(pa[:], (kn[:, 0:128] if os.environ.get("SBKN", "0") == "1" else trib[:]), (qn[:, qtok] if os.environ.get("SBRQ", "0") == "1" else (V(qn[:, qtok].ap, spb[i].key) if os.environ.get("SBRQ", "0") == "2" else spb[i][:])), start=True, stop=(first or os.environ.get("SBNO2", "0") != "0"))
                    if not first and os.environ.get("SBNO2", "0") == "0":
                        b.mm(pa[:], onesb[:], ssumb[i][:], start=False, stop=True)
                    if STEP < 4:
                        first = False
                        continue
                    b.tt('dve', arg[i][:], zs[i][:], pa[:], ALU.subtract)
                    b.act(E[i][:], arg[i][:], AF.Exp)
                    if diag:
                        b.tt(PE_, E[i][:], E[i][:], m, ALU.mult)
                    if STEP < 5:
                        first = False
                        continue
                    b.mm(ps_o[:], vt[:, kb, :], E[i][:], start=first, stop=(kb == 0))
                    if STEP < 6:
                        first = False
                        continue
                    if kb > 0:
                        if first:
                            b.cp(PE_, ssum[:], sp[i][:])
                        else:
                            b.tt(PE_, ssum[:], ssum[:], sp[i][:], ALU.add)
                        b.cp(PE_, ssumb[(i + 1) % 2][:], ssum[:])
                    first = False
                o_ = ot[Q % 2]
                b.cp('act', o_[:], ps_o[:])
                r0 = cf.RW + cf.CW + h * 128
                b.dma(V(yT[r0:r0 + 128, qtok], ("yTs", h, Q)), o_[:])
        b.P.barrier()
        b.P.flush()


def ph_rwkv(ctx):
    b, cf, C, PR = ctx['b'], ctx['cf'], ctx['C'], ctx['PR']
    T, H, RW = cf.T, cf.H, cf.RW
    pT, yT = ctx['pT'], ctx['yT']
    identb = ctx['identb']
    TW = 512
    NS = TW // 128
    b.P.barrier()
    with contextlib.ExitStack() as st:
        sb = lambda n, s, d=F32: b.sb(st, n, s, d)
        wup = sb("r_wup", [96, RW], BF16)
        aup = sb("r_aup", [96, RW], BF16)
        gup = sb("r_gup", [128, 2, RW], BF16)
        with contextlib.ExitStack() as stw:
            Wl = ctx['W']
            f1 = b.sb(stw, "r_f1", [96, RW], F32)
            f2 = b.sb(stw, "r_f2", [96, RW], F32)
            f3 = b.sb(stw, "r_f3", [128, 2 * RW], F32)
            b.dma(f1[:], V(Wl['w_up'], "in_wup"))
            b.dma(f2[:], V(Wl['a_up'], "in_aup"))
            b.dma(f3[:], V(Wl['g_up'], "in_gup"))
            b.cp('dve', wup[:], f1[:])
            b.cp('dve', aup[:], f2[:])
            b.cp('dve', gup[:].f(lambda a: a.rearrange("p c n -> p (c n)")), f3[:])
            b.P.barrier()
            b.P.flush()
        ST32 = sb("r_st32", [64, H, 64])
        STb = sb("r_stb", [64, H, 64], BF16)
        b.memset('dve', ST32[:], 0.0)
        b.memset('dve', STb[:], 0.0)
        lw_in = sb("r_lwin", [96, TW + 1])
        la_in = sb("r_lain", [96, TW + 1])
        lg_in = sb("r_lgin", [128, 2, TW + 1])
        d96 = sb("r_d96", [96, TW])
        d128 = sb("r_d128", [128, 2, TW])
        twl = sb("r_twl", [96, TW], BF16)
        al = sb("r_al", [96, TW], BF16)
        gl = sb("r_gl", [128, 2, TW], BF16)
        rkv_in = sb("r_rkvin", [64, 3, TW + 1])
        dd = sb("r_dd", [64, 3, TW])
        rkv = sb("r_rkv", [64, 3, TW])
        sig = sb("r_sig", [64, TW])
        cum = sb("r_cum", [64, TW])
        cumx = sb("r_cumx", [64, TW])
        eg = sb("r_eg", [64, TW])
        egx = sb("r_egx", [64, TW])
        einv = sb("r_einv", [64, TW])
        iclr = sb("r_iclr", [64, TW])
        gsb = sb("r_gsb", [64, TW])
        kkr = sb("r_kkr", [64, TW])
        sqk = sb("r_sqk", [64, TW])
        rsk = sb("r_rsk", [64, TW])
        kk = sb("r_kk", [64, TW])
        tmp = sb("r_tmp", [64, TW])
        kt = sb("r_kt", [64, TW])
        bb = sb("r_bb", [64, TW])
        pr2 = sb("r_pr2", [64, TW])
        bonus = sb("r_bonus", [64, TW])
        ART = sb("r_art", [64, 2, TW], BF16)
        BH = sb("r_bh", [64, TW], BF16)
        KH = sb("r_kh", [64, TW], BF16)
        BE = sb("r_be", [64, TW], BF16)
        KE = sb("r_ke", [64, TW], BF16)
        VB = sb("r_vb", [64, TW], BF16)
        TOK = sb("r_tok", [128, NS, 4, 64], BF16)
        TOKc = [sb("r_tokc%d" % i, [128, NS, 2, 64], BF16) for i in range(2)]
        Pm = [sb("r_P%d" % i, [128, 128], BF16) for i in range(2)]
        PTm = [sb("r_PT%d" % i, [128, 128], BF16) for i in range(2)]
        X2 = sb("r_x2", [128, 2, 128], BF16)
        X3 = sb("r_x3", [128, 2, 128], BF16)
        R = [sb("r_R%d" % i, [128, 128], BF16) for i in range(2)]
        rqT = sb("r_rqT", [64, 128], BF16)
        Wc = [sb("r_Wc%d" % i, [64, 64], BF16) for i in range(2)]
        O = sb("r_O", [64, TW])
        osq = sb("r_osq", [64, TW])
        mean = sb("r_mean", [64, TW])
        msq = sb("r_msq", [64, TW])
        rstd = sb("r_rstd", [64, TW])
        yo = sb("r_yo", [64, TW], BF16)
        ps_w = b.ps(st, "r_psw", [128, 512])
        ps_w2 = b.ps(st, "r_psw2", [128, 512])
        ps_t = b.ps(st, "r_pst", [128, NS, 4, 64], BF16)
        ps_gg = b.ps(st, "r_psgg", [128, 2, 256])
        ps_g = [Tv(ps_gg.h[:, i, :], ps_gg.key) for i in range(2)]
        ps_pr = b.ps(st, "r_pspr", [128, 4, 128])
        ps_p = [Tv(ps_pr.h[:, i, :], ps_pr.key) for i in range(2)]
        ps_r = b.ps(st, "r_psr", [128, 512])
        ps_q = b.ps(st, "r_psq", [128, 512])
        ps_o = b.ps(st, "r_pso", [64, 512])
        ones = C['ones']
        cmask = C['cmask']

        def shift(out, xin, dtmp, mu_col):
            pass

        import os
        RST = int(os.environ.get("RSTEP", "99"))
        RSP = int(os.environ.get("RSPAN", "99"))
        for tt in range(T // TW):
            t0 = tt * TW
            for (tin, r0, nr) in ((lw_in, cf.o_wlo, 96), (la_in, cf.o_alo, 96)):
                if tt == 0:
                    b.memset('pool', tin[:, 0:1], 0.0)
                    b.dma(tin[:, 1:], V(pT[r0:r0 + nr, 0:TW], ("pTl", r0, tt)))
                else:
                    b.dma(tin[:], V(pT[r0:r0 + nr, t0 - 1:t0 + TW], ("pTl", r0, tt)))
            for c in range(2):
                r0 = cf.o_glo + c * 128
                if tt == 0:
                    b.memset('pool', lg_in[:, c, 0:1], 0.0)
                    b.dma(lg_in[:, c, 1:], V(pT[r0:r0 + 128, 0:TW], ("pTl", r0, tt)))
                else:
                    b.dma(lg_in[:, c, :], V(pT[r0:r0 + 128, t0 - 1:t0 + TW], ("pTl", r0, tt)))
            b.tt('dve', d96[:], lw_in[:, 0:TW], lw_in[:, 1:TW + 1], ALU.subtract)
            b.stt(d96[:], d96[:], PR['mu_w'][:, 0:1], lw_in[:, 1:TW + 1], ALU.mult, ALU.add)
            b.act(twl[:], d96[:], AF.Tanh)
            b.tt('dve', d96[:], la_in[:, 0:TW], la_in[:, 1:TW + 1], ALU.subtract)
            b.stt(al[:], d96[:], PR['mu_a'][:, 0:1], la_in[:, 1:TW + 1], ALU.mult, ALU.add)
            for c in range(2):
                b.tt('dve', d128[:, c, :], lg_in[:, c, 0:TW], lg_in[:, c, 1:TW + 1], ALU.subtract)
                b.stt(d128[:, c, :], d128[:, c, :], PR['mu_g'][:, c:c + 1], lg_in[:, c, 1:TW + 1], ALU.mult, ALU.add)
                b.act(gl[:, c, :], d128[:, c, :], AF.Sigmoid)
            for h in range(H):
                hc = slice(h * 64, (h + 1) * 64)
                col = lambda p_: PR[p_][:, h:h + 1]
                for s in range(3):
                    r0 = s * RW + h * 64
                    if tt == 0:
                        b.memset('pool', rkv_in[:, s, 0:1], 0.0)
                        b.dma(rkv_in[:, s, 1:], V(pT[r0:r0 + 64, 0:TW], ("pTr", s, h, tt)))
                    else:
                        b.dma(rkv_in[:, s, :], V(pT[r0:r0 + 64, t0 - 1:t0 + TW], ("pTr", s, h, tt)))
                b.tt('dve', dd[:], rkv_in[:, :, 0:TW], rkv_in[:, :, 1:TW + 1], ALU.subtract)
                for s in range(3):
                    b.stt(rkv[:, s, :], dd[:, s, :], PR['mu_rkv'][:, s * H + h:s * H + h + 1],
                          rkv_in[:, s, 1:TW + 1], ALU.mult, ALU.add)
                r_, k_, v_ = rkv[:, 0, :], rkv[:, 1, :], rkv[:, 2, :]
                b.mm(ps_w[0:64, :], wup[:, hc], twl[:])
                b.act(sig[:], ps_w[0:64, :], AF.Sigmoid, bias=col('w0'))
                b.scan(cum[:], cmask[:], sig[:], 0.0, ALU.mult, ALU.add)
                b.tt('pool', cumx[:], cum[:], sig[:], ALU.subtract)
                b.act(eg[:], cum[:], AF.Exp, scale=CDEC)
                b.act(egx[:], cumx[:], AF.Exp, scale=CDEC)
                b.act(einv[:], cum[:], AF.Exp, scale=-CDEC)
                b.mm(ps_w2[0:64, :], aup[:, hc], al[:])
                b.act(iclr[:], ps_w2[0:64, :], AF.Sigmoid, bias=col('a0'))
                b.mm(ps_w[0:64, :], gup[:, 0, hc], gl[:, 0, :], start=True, stop=False)
                b.mm(ps_w[0:64, :], gup[:, 1, hc], gl[:, 1, :], start=False, stop=True)
                b.cp('act', gsb[:], ps_w[0:64, :])
                b.ts('dve', kkr[:], k_, col('k_k'), None, ALU.mult)
                b.act(sqk[:], kkr[:], AF.Square)
                b.mm(ps_w2[0:64, :], ones[0:64, 0:64], sqk[:])
                b.ts('dve', rsk[:], ps_w2[0:64, :], 1e-24, None, ALU.max)
                b.rsqrt(rsk[:], rsk[:])
                b.tt('dve', kk[:], kkr[:], rsk[:], ALU.mult)
                b.ts('dve', tmp[:], iclr[:], -1.0, col('k_a'), ALU.add, ALU.mult)
                b.stt(kt[:], tmp[:], 1.0, k_, ALU.add, ALU.mult)
                b.tt('pool', bb[:], kk[:], iclr[:], ALU.mult)
                b.stt(pr2[:], r_, col('r_k'), kt[:], ALU.mult, ALU.mult)
                b.mm(ps_w[0:64, :], ones[0:64, 0:64], pr2[:])
                b.tt('dve', bonus[:], ps_w[0:64, :], v_, ALU.mult)
                if RST < 1:
                    continue
                b.stt(ART[:, 0, :], kk[:], -1.0, egx[:], ALU.mult, ALU.mult)
                b.tt('pool', ART[:, 1, :], r_, eg[:], ALU.mult)
                b.tt('dve', BH[:], bb[:], einv[:], ALU.mult)
                b.tt('pool', KH[:], kt[:], einv[:], ALU.mult)
                b.cp('pool', VB[:], v_)
                for cch in range(TW // 64):
                    cs = slice(cch * 64, (cch + 1) * 64)
                    gcol = eg[:, cch * 64 + 63:cch * 64 + 64]
                    b.ts('dve', BE[:, cs], BH[:, cs], gcol, None, ALU.mult)
                    b.ts('pool', KE[:, cs], KH[:, cs], gcol, None, ALU.mult)
                if RST < 2:
                    continue
                for s_ in range(NS):
                    sc = slice(s_ * 128, (s_ + 1) * 128)
                    for qi, src in enumerate((ART[:, 0, sc], VB[:, sc], BE[:, sc], KE[:, sc])):
                        b.tr(ps_t[:, s_, qi, :], src, identb[0:64, 0:64])
                b.cp('act', TOK[:], ps_t[:])
                for c2 in range(2):
                    b.ts('dve' if c2 == 0 else 'pool', TOKc[c2][:], TOK[:, :, 2:4, :], C['mrow'][:, c2:c2 + 1], None, ALU.mult)
                if RST < 3:
                    continue
                for s_ in range(NS if RST >= 9 else 0):
                    sc = slice(s_ * 128, (s_ + 1) * 128)
                    a_tok, v_tok = TOK[:, s_, 0, :], TOK[:, s_, 1, :]
                    b.mm(ps_p[0][:], ART[:, 0, sc], BH[:, sc])
                    b.tt('dve', Pm[0][:], ps_p[0][:], C['m_sl'][:], ALU.mult)
                    b.mm(ps_g[0][:], BH[:, sc], ART[:, :, sc])
                    b.tt('dve', X2[:].f(lambda a: a.rearrange("p a t -> p (a t)")), ps_g[0][:], C['m_2'][:], ALU.mult)
                    b.mm(ps_g[1][:], KH[:, sc], ART[:, :, sc])
                    b.tt('dve', X3[:].f(lambda a: a.rearrange("p a t -> p (a t)")), ps_g[1][:], C['m_2'][:], ALU.mult)
                    b.cp('pool', PTm[0][:], X2[:, 0, :])
                    if RSP < 2:
                        continue
                    b.mm(ps_r[:, 0:64], X3[:, 0, :], v_tok)
                    b.cp('pool', R[0][:, 0:64], a_tok)
                    b.cp('act', R[0][:, 64:128], ps_r[:, 0:64])
                    cur = 0
                    if RSP < 3:
                        continue
                    for lev in range(6):
                        Pc, PTc = Pm[lev % 2], PTm[lev % 2]
                        b.mm(ps_r[:, 0:128], identb[:], R[cur][:], start=True, stop=False)
                        b.mm(ps_r[:, 0:128], PTc[:], R[cur][:], start=False, stop=True)
                        b.cp('act', R[1 - cur][:], ps_r[:, 0:128])
                        cur = 1 - cur
                        if lev < 5:
                            Pn, PTn = Pm[(lev + 1) % 2], PTm[(lev + 1) % 2]
                            b.mm(ps_p[0][:], PTc[:], Pc[:])
                            b.mm(ps_p[1][:], Pc[:], PTc[:])
                            b.cp('dve', Pn[:], ps_p[0][:])
                            b.cp('dve', PTn[:], ps_p[1][:])
                    X = R[cur]
                    if RSP < 4:
                        continue
                    b.mm(ps_q[0:64, 0:128], X[:, 0:64], X2[:, 1, :])
                    b.tt('dve', rqT[:], ps_q[0:64, 0:128], ART[:, 1, sc], ALU.add)
                    for c2 in range((int(os.environ.get('RC2', '2'))) if RSP >= 5 else 0):
                        cb_ = slice(c2 * 64, (c2 + 1) * 64)
                        be_c = TOKc[c2][:, s_, 0, :]
                        ke_c = TOKc[c2][:, s_, 1, :]
                        oc = slice(s_ * 128 + c2 * 64, s_ * 128 + (c2 + 1) * 64)
                        W_ = Wc[c2]
                        b.mm(ps_p[c2][0:64, 0:64], X[:, 0:64], be_c)
                        b.cp('dve', W_[:], ps_p[c2][0:64, 0:64])
                        RC2S = int(os.environ.get("RC2S", "9"))
                        if c2 == 1 and RC2S < 2:
                            continue
                        b.mm(ps_o[:, oc], X[:, 64:128], X2[:, 1, cb_], start=True, stop=False)
                        b.mm(ps_o[:, oc], v_tok, X3[:, 1, cb_], start=False, stop=False)
                        b.mm(ps_o[:, oc], STb[:, h, :], rqT[:, cb_], start=False, stop=True)
                        if c2 == 1 and RC2S < 3:
                            continue
                        b.mm(ps_g[c2][0:64, 0:64], W_[:], STb[:, h, :], start=True, stop=False)
                        b.mm(ps_g[c2][0:64, 0:64], be_c, X[:, 64:128], start=False, stop=False)
                        b.mm(ps_g[c2][0:64, 0:64], ke_c, v_tok, start=False, stop=True)
                        gcol = eg[:, s_ * 128 + c2 * 64 + 63:s_ * 128 + c2 * 64 + 64]
                        b.stt(ST32[:, h, :], ST32[:, h, :], gcol, ps_g[c2][0:64, 0:64], ALU.mult, ALU.add)
                        b.cp('act', STb[:, h, :], ST32[:, h, :])
                b.cp('act', O[:], ps_o[:])
                b.mm(ps_w[0:64, :], ones[0:64, 0:64], O[:])
                b.act(osq[:], O[:], AF.Square)
                b.mm(ps_w2[0:64, :], ones[0:64, 0:64], osq[:])
                b.ts('dve', mean[:], ps_w[0:64, :], 1.0 / 64, None, ALU.mult)
                b.tt('dve', msq[:], mean[:], mean[:], ALU.mult)
                b.stt(rstd[:], ps_w2[0:64, :], 1.0 / 64, msq[:], ALU.mult, ALU.subtract)
                b.ts('dve', rstd[:], rstd[:], GN_EPS, None, ALU.add)
                b.rsqrt(rstd[:], rstd[:])
                b.tt('dve', O[:], O[:], mean[:], ALU.subtract)
                b.tt('dve', O[:], O[:], rstd[:], ALU.mult)
                b.ts('dve', O[:], O[:], col('gn_g'), col('gn_b'), ALU.mult, ALU.add)
                b.tt('dve', O[:], O[:], bonus[:], ALU.add)
                b.tt('dve', yo[:], O[:], gsb[:], ALU.mult)
                b.dma(V(yT[h * 64:(h + 1) * 64, t0:t0 + TW], ("yTr", h, tt)), yo[:])
        b.P.barrier()
        b.P.flush()


_NC_CACHE = {}


def run_layers(cf, x, inp, layers, fused, debug=False, stages=None):
    Bn = x.shape[0]
    consts = const_arrays()
    f = lambda a: np.ascontiguousarray(np.asarray(a, dtype=np.float32))
    groups = [layers] if fused else [[l] for l in layers]
    dbg_out = None
    for grp in groups:
        key = (cf.D, cf.T, len(grp), debug, None if stages is None else tuple(stages))
        if key not in _NC_CACHE:
            _NC_CACHE[key] = build(cf, len(grp), debug=debug, stages=stages)
        nc = _NC_CACHE[key]
        shared = {}
        for i, l in enumerate(grp):
            shared["w_in%d" % i] = f(inp['w_in'][l])
            shared["w_out%d" % i] = f(inp['w_out'][l])
            shared["ffn_up%d" % i] = f(inp['ffn_up'][l])
            shared["ffn_down%d" % i] = f(inp['ffn_down'][l])
            for n, a in prep_params(cf, l, inp).items():
                shared["%s%d" % (n, i)] = a
        for n, a in consts.items():
            shared["c_" + n] = a
        in_maps = []
        for bi in range(Bn):
            m = dict(shared)
            m["x"] = f(x[bi])
            in_maps.append(m)
        res = run_bass_kernel_spmd(nc, in_maps, core_ids=list(range(Bn)))
        x = np.stack([np.asarray(res.results[bi]["out"]) for bi in range(Bn)], 0)
        if debug:
            dbg_out = [{k: np.asarray(v) for k, v in res.results[bi].items()} for bi in range(Bn)]
    return x, dbg_out


FUSED = False


def kernel(**inputs):
    x = np.asarray(inputs['x'], dtype=np.float32)
    Bn, T, D = x.shape
    depth = inputs['w_in'].shape[0]
    cf = Cfg(D, T)
    out, _ = run_layers(cf, x, inputs, list(range(depth)), FUSED)
    return out.astype(np.float32)
```
